# Optimizing a Trainium2 kernel written in Bass

```python
import jax, jax.numpy as jnp
from jax import lax
import numpy as np

D_MODEL = 1024
BATCH = 16
SEQ = 2048
DEPTH = 4

HEAD_DIM = D_MODEL // 16
NSA_HEADS = 8
NSA_KV_GROUPS = 2
FOX_HEADS = 4
MOBA_HEADS = 4
MIX_WIDTH = (NSA_HEADS + FOX_HEADS + MOBA_HEADS) * HEAD_DIM
ROPE_DIM = HEAD_DIM // 4
ROPE_THETA = 500000.0
CMP_LEN = 32
CMP_STRIDE = 16
CMP_HIDDEN = 4 * HEAD_DIM
SLC_LEN = 64
SLC_TOPN = 8
WIN = 512
MOBA_BLOCK = 256
MOBA_TOPK = 3
QBLOCK = 128
GATHER_CHUNK = 32
D_FF = 256 * ((8 * D_MODEL // 3 + 255) // 256)
FOX_FGATE_BIAS = 3.0
MAX_POS_OFFSET = 4096
EPS = 1e-6

_NSA_KV = NSA_KV_GROUPS * HEAD_DIM
IN_SPLITS = (
    ("nsa_q", NSA_HEADS * HEAD_DIM),
    ("nsa_kc", _NSA_KV), ("nsa_vc", _NSA_KV),
    ("nsa_ks", _NSA_KV), ("nsa_vs", _NSA_KV),
    ("nsa_kw", _NSA_KV), ("nsa_vw", _NSA_KV),
    ("nsa_gate", 3 * NSA_HEADS),
    ("fox_q", FOX_HEADS * HEAD_DIM), ("fox_k", FOX_HEADS * HEAD_DIM),
    ("fox_v", FOX_HEADS * HEAD_DIM), ("fox_f", FOX_HEADS),
    ("moba_q", MOBA_HEADS * HEAD_DIM), ("moba_k", MOBA_HEADS * HEAD_DIM),
    ("moba_v", MOBA_HEADS * HEAD_DIM),
)
IN_COLS = sum(w for _, w in IN_SPLITS)

kernel_name = "hymba_nsa_fox_moba_macaron_adaln"


def rms_norm(x, g):
    xf = x.astype(jnp.float32)
    y = xf * lax.rsqrt(jnp.mean(xf * xf, axis=-1, keepdims=True) + EPS)
    return (y * g.astype(jnp.float32)).astype(x.dtype)


def ada_norm(x, g, shift, scale):
    return rms_norm(x, g) * (1 + scale[:, None, :]) + shift[:, None, :]


def swiglu(h, w13, w2):
    a, b = jnp.split(h @ w13, 2, axis=-1)
    return (jax.nn.silu(a) * b) @ w2


def masked_softmax(s, mask):
    s = jnp.where(mask, s.astype(jnp.float32), -jnp.inf)
    m = jnp.max(s, axis=-1, keepdims=True)
    m = jnp.where(jnp.isfinite(m), m, 0.0)
    e = jnp.exp(s - m)
    return e / jnp.maximum(jnp.sum(e, axis=-1, keepdims=True), 1e-30)


def rope_tables(positions):
    inv = ROPE_THETA ** (-jnp.arange(0, ROPE_DIM, 2, dtype=jnp.float32) / ROPE_DIM)
    ang = positions.astype(jnp.float32)[..., None] * inv
    return jnp.cos(ang), jnp.sin(ang)


def apply_rope(x, cos, sin):
    r, rest = x[..., :ROPE_DIM], x[..., ROPE_DIM:]
    r1, r2 = r[..., :ROPE_DIM // 2], r[..., ROPE_DIM // 2:]
    c, s = cos[:, :, None, :], sin[:, :, None, :]
    rot = jnp.concatenate([r1 * c - r2 * s, r2 * c + r1 * s], axis=-1).astype(x.dtype)
    return jnp.concatenate([rot, rest], axis=-1)


def split_blocks(a, axis, size):
    n = a.shape[axis] // size
    a = a.reshape(a.shape[:axis] + (n, size) + a.shape[axis + 1:])
    return jnp.moveaxis(a, axis, 0)


def merge_blocks(a, axis):
    a = jnp.moveaxis(a, 0, axis)
    return a.reshape(a.shape[:axis] + (a.shape[axis] * a.shape[axis + 1],) + a.shape[axis + 2:])


def nsa_mixer(q, kc, vc, ks, vs, kw, vw, gates, cmp_pos, cmp_w1, cmp_w2):
    B, S = q.shape[0], q.shape[1]
    G, R, D = NSA_KV_GROUPS, NSA_HEADS // NSA_KV_GROUPS, HEAD_DIM
    scale = D ** -0.5
    qg = q.reshape(B, S, G, R, D).transpose(0, 2, 3, 1, 4)
    t = np.arange(S)

    n_c = (S - CMP_LEN) // CMP_STRIDE + 1
    win_idx = np.arange(n_c)[:, None] * CMP_STRIDE + np.arange(CMP_LEN)[None, :]

    def compress(z, pos, w1, w2):
        zb = z[:, win_idx] + pos[None, None, :, None, :]
        zb = zb.transpose(0, 3, 1, 2, 4).reshape(B, G, n_c, CMP_LEN * D)
        return jax.nn.gelu(zb @ w1) @ w2

    kcmp = compress(kc, cmp_pos[0], cmp_w1[0], cmp_w2[0])
    vcmp = compress(vc, cmp_pos[1], cmp_w1[1], cmp_w2[1])
    cmp_end = np.arange(n_c) * CMP_STRIDE + CMP_LEN - 1
    cmp_mask = cmp_end[None, :] <= t[:, None]
    s_cmp = jnp.einsum('bgrsd,bgcd->bgrsc', qg, kcmp) * scale
    p_cmp = masked_softmax(s_cmp, cmp_mask)
    o_cmp = jnp.einsum('bgrsc,bgcd->bgrsd', p_cmp.astype(vcmp.dtype), vcmp)

    n_s = S // SLC_LEN
    c0 = np.arange(n_c)[:, None] * CMP_STRIDE
    j0 = np.arange(n_s)[None, :] * SLC_LEN
    overlap = ((c0 < j0 + SLC_LEN) & (c0 + CMP_LEN > j0)).astype(np.float32)
    imp = jnp.einsum('bgrsc,cj->bgsj', p_cmp, jnp.asarray(overlap))
    tb = (t // SLC_LEN)[:, None]
    jj = np.arange(n_s)[None, :]
    valid = jj <= tb
    forced = (jj == 0) | (jj == tb) | (jj == tb - 1)
    imp = jnp.where(valid, jnp.where(forced, jnp.inf, imp), -jnp.inf)
    n_top = min(SLC_TOPN, n_s)
    top_val, top_idx = lax.top_k(imp, n_top)
    top_ok = top_val > -jnp.inf

    ks_blk = ks.reshape(B, n_s, SLC_LEN, G, D).transpose(0, 3, 1, 2, 4)
    vs_blk = vs.reshape(B, n_s, SLC_LEN, G, D).transpose(0, 3, 1, 2, 4)
    C = GATHER_CHUNK
    bi = jnp.arange(B)[:, None, None]
    gi = jnp.arange(G)[None, :, None]
    k_sel = n_top * SLC_LEN

    def slc_chunk(args):
        qc, idx, ok, q0 = args
        flat = idx.reshape(B, G, C * n_top)
        kg = ks_blk[bi, gi, flat].reshape(B, G, C, k_sel, D)
        vg = vs_blk[bi, gi, flat].reshape(B, G, C, k_sel, D)
        kpos = (idx[..., None] * SLC_LEN + jnp.arange(SLC_LEN)).reshape(B, G, C, k_sel)
        qpos = q0 + jnp.arange(C)
        mask = jnp.repeat(ok, SLC_LEN, axis=-1) & (kpos <= qpos[:, None])
        s = jnp.einsum('bgrcd,bgckd->bgrck', qc, kg) * scale
        p = masked_softmax(s, mask[:, :, None])
        return jnp.einsum('bgrck,bgckd->bgrcd', p.astype(vg.dtype), vg)

    o_slc = lax.map(slc_chunk, (split_blocks(qg, 3, C), split_blocks(top_idx, 2, C),
                                split_blocks(top_ok, 2, C), jnp.arange(S // C) * C))
    o_slc = merge_blocks(o_slc, 3)

    kw_p = jnp.pad(kw.transpose(0, 2, 1, 3), ((0, 0), (0, 0), (WIN, 0), (0, 0)))
    vw_p = jnp.pad(vw.transpose(0, 2, 1, 3), ((0, 0), (0, 0), (WIN, 0), (0, 0)))
    span = WIN + QBLOCK

    def win_block(args):
        qb, q0 = args
        kb = lax.dynamic_slice_in_dim(kw_p, q0, span, axis=2)
        vb = lax.dynamic_slice_in_dim(vw_p, q0, span, axis=2)
        qpos = q0 + jnp.arange(QBLOCK)
        kpos = q0 - WIN + jnp.arange(span)
        d = qpos[:, None] - kpos[None, :]
        mask = (kpos[None, :] >= 0) & (d >= 0) & (d < WIN)
        s = jnp.einsum('bgrqd,bgkd->bgrqk', qb, kb) * scale
        p = masked_softmax(s, mask)
        return jnp.einsum('bgrqk,bgkd->bgrqd', p.astype(vb.dtype), vb)

    o_win = lax.map(win_block, (split_blocks(qg, 3, QBLOCK), jnp.arange(S // QBLOCK) * QBLOCK))
    o_win = merge_blocks(o_win, 3)

    g = gates.reshape(B, S, G, R, 3).transpose(0, 2, 3, 1, 4)
    o = g[..., 0:1] * o_cmp + g[..., 1:2] * o_slc + g[..., 2:3] * o_win
    return o.transpose(0, 3, 1, 2, 4).reshape(B, S, NSA_HEADS * D)


def fox_mixer(q, k, v, f_logit, f_bias):
    B, S, H, D = q.shape
    scale = D ** -0.5
    logf = jax.nn.log_sigmoid((f_logit + f_bias).astype(jnp.float32))
    F = jnp.cumsum(logf, axis=1).transpose(0, 2, 1)
    qt, kt, vt = (a.transpose(0, 2, 1, 3) for a in (q, k, v))
    kpos = jnp.arange(S)

    def blk(args):
        qb, Fq, q0 = args
        s = (jnp.einsum('bhqd,bhkd->bhqk', qb, kt).astype(jnp.float32) * scale
             + Fq[..., None] - F[:, :, None, :])
        qpos = q0 + jnp.arange(QBLOCK)
        p = masked_softmax(s, kpos[None, :] <= qpos[:, None])
        return jnp.einsum('bhqk,bhkd->bhqd', p.astype(vt.dtype), vt)

    o = lax.map(blk, (split_blocks(qt, 2, QBLOCK), split_blocks(F, 2, QBLOCK),
                      jnp.arange(S // QBLOCK) * QBLOCK))
    o = merge_blocks(o, 2)
    return o.transpose(0, 2, 1, 3).reshape(B, S, H * D)


def moba_mixer(q, k, v):
    B, S, H, D = q.shape
    scale = D ** -0.5
    qt, kt, vt = (a.transpose(0, 2, 1, 3) for a in (q, k, v))
    NB = -(-S // MOBA_BLOCK)
    Sp = NB * MOBA_BLOCK
    kp = jnp.pad(kt, ((0, 0), (0, 0), (0, Sp - S), (0, 0)))
    vp = jnp.pad(vt, ((0, 0), (0, 0), (0, Sp - S), (0, 0)))
    kb = kp.reshape(B, H, NB, MOBA_BLOCK, D)
    vb = vp.reshape(B, H, NB, MOBA_BLOCK, D)
    n_sel = min(MOBA_TOPK, NB - 1)
    C = GATHER_CHUNK
    t = np.arange(S)
    xs = (split_blocks(qt, 2, C), jnp.arange(S // C) * C)
    if n_sel > 0:
        kmean = jnp.mean(kb.astype(jnp.float32), axis=3)
        gate = jnp.einsum('bhsd,bhnd->bhsn', qt.astype(jnp.float32), kmean)
        past = np.arange(NB)[None, :] < (t // MOBA_BLOCK)[:, None]
        gate = jnp.where(past, gate, -jnp.inf)
        top_val, top_idx = lax.top_k(gate, n_sel)
        xs = xs + (split_blocks(top_idx, 2, C), split_blocks(top_val > -jnp.inf, 2, C))
    bi = jnp.arange(B)[:, None, None]
    hi = jnp.arange(H)[None, :, None]
    k_sel = n_sel * MOBA_BLOCK

    def chunk(args):
        qc, q0 = args[0], args[1]
        own = q0 // MOBA_BLOCK
        k_own = lax.dynamic_slice_in_dim(kp, own * MOBA_BLOCK, MOBA_BLOCK, axis=2)
        v_own = lax.dynamic_slice_in_dim(vp, own * MOBA_BLOCK, MOBA_BLOCK, axis=2)
        qpos = q0 + jnp.arange(C)
        own_pos = own * MOBA_BLOCK + jnp.arange(MOBA_BLOCK)
        s_own = jnp.einsum('bhcd,bhkd->bhck', qc, k_own) * scale
        m_own = jnp.broadcast_to(own_pos[None, :] <= qpos[:, None], s_own.shape)
        if n_sel == 0:
            p = masked_softmax(s_own, m_own)
            return jnp.einsum('bhck,bhkd->bhcd', p.astype(v_own.dtype), v_own)
        idx, ok = args[2], args[3]
        flat = idx.reshape(B, H, C * n_sel)
        kg = kb[bi, hi, flat].reshape(B, H, C, k_sel, D)
        vg = vb[bi, hi, flat].reshape(B, H, C, k_sel, D)
        s_sel = jnp.einsum('bhcd,bhckd->bhck', qc, kg) * scale
        m_sel = jnp.repeat(ok, MOBA_BLOCK, axis=-1)
        p = masked_softmax(jnp.concatenate([s_sel, s_own], axis=-1),
                           jnp.concatenate([m_sel, m_own], axis=-1))
        p = p.astype(vg.dtype)
        return (jnp.einsum('bhck,bhckd->bhcd', p[..., :k_sel], vg)
                + jnp.einsum('bhck,bhkd->bhcd', p[..., k_sel:], v_own))

    o = merge_blocks(lax.map(chunk, xs), 2)
    return o.transpose(0, 2, 1, 3).reshape(B, S, H * D)


def token_mixer(h, w_in, f_bias, cmp_pos, cmp_w1, cmp_w2, w_out, cos, sin):
    B, S, _ = h.shape
    proj = h @ w_in
    parts = {}
    off = 0
    for name, width in IN_SPLITS:
        parts[name] = proj[..., off:off + width]
        off += width

    def heads(a, n):
        return a.reshape(B, S, n, HEAD_DIM)

    o_nsa = nsa_mixer(
        apply_rope(heads(parts['nsa_q'], NSA_HEADS), cos, sin),
        apply_rope(heads(parts['nsa_kc'], NSA_KV_GROUPS), cos, sin), heads(parts['nsa_vc'], NSA_KV_GROUPS),
        apply_rope(heads(parts['nsa_ks'], NSA_KV_GROUPS), cos, sin), heads(parts['nsa_vs'], NSA_KV_GROUPS),
        apply_rope(heads(parts['nsa_kw'], NSA_KV_GROUPS), cos, sin), heads(parts['nsa_vw'], NSA_KV_GROUPS),
        jax.nn.sigmoid(parts['nsa_gate']).reshape(B, S, NSA_HEADS, 3),
        cmp_pos, cmp_w1, cmp_w2)
    o_fox = fox_mixer(heads(parts['fox_q'], FOX_HEADS), heads(parts['fox_k'], FOX_HEADS),
                      heads(parts['fox_v'], FOX_HEADS), parts['fox_f'], f_bias)
    o_moba = moba_mixer(apply_rope(heads(parts['moba_q'], MOBA_HEADS), cos, sin),
                        apply_rope(heads(parts['moba_k'], MOBA_HEADS), cos, sin),
                        heads(parts['moba_v'], MOBA_HEADS))
    return jnp.concatenate([o_nsa, o_fox, o_moba], axis=-1) @ w_out


def setup_inputs(seed: int = 0) -> dict:
    key = jax.random.key(seed)
    ks = jax.random.split(key, 16)
    D = D_MODEL

    def nrm(k, shape, s):
        return jax.random.normal(k, shape, jnp.float32) * s

    x = nrm(ks[0], (BATCH, SEQ, D), 1.0)
    c = nrm(ks[1], (BATCH, D), 1.0)
    offset = jax.random.randint(ks[2], (BATCH, 1), 0, MAX_POS_OFFSET, dtype=jnp.int32)
    positions = offset + jnp.arange(SEQ, dtype=jnp.int32)[None, :]
    return {
        "x": x,
        "c": c,
        "positions": positions,
        "norm_g": 1.0 + nrm(ks[3], (DEPTH, 3, D), 0.05),
        "w_ada": nrm(ks[4], (DEPTH, D, 9 * D), 0.5 * D ** -0.5),
        "b_ada": nrm(ks[5], (DEPTH, 9 * D), 0.01),
        "w_in": nrm(ks[6], (DEPTH, D, IN_COLS), D ** -0.5),
        "fox_fbias": FOX_FGATE_BIAS + nrm(ks[7], (DEPTH, FOX_HEADS), 0.1),
        "cmp_pos": nrm(ks[8], (DEPTH, 2, CMP_LEN, HEAD_DIM), 0.1),
        "cmp_w1": nrm(ks[9], (DEPTH, 2, CMP_LEN * HEAD_DIM, CMP_HIDDEN), (CMP_LEN * HEAD_DIM) ** -0.5),
        "cmp_w2": nrm(ks[10], (DEPTH, 2, CMP_HIDDEN, HEAD_DIM), CMP_HIDDEN ** -0.5),
        "w_out": nrm(ks[11], (DEPTH, MIX_WIDTH, D), MIX_WIDTH ** -0.5),
        "ffn_w13": nrm(ks[12], (DEPTH, 2, D, 2 * D_FF), D ** -0.5),
        "ffn_w2": nrm(ks[13], (DEPTH, 2, D_FF, D), D_FF ** -0.5),
        "final_g": 1.0 + nrm(ks[14], (D,), 0.05),
    }


def reference(x, c, positions, norm_g, w_ada, b_ada, w_in, fox_fbias, cmp_pos, cmp_w1, cmp_w2,
              w_out, ffn_w13, ffn_w2, final_g):
    cos, sin = rope_tables(positions)
    c_act = jax.nn.silu(c)
    B = x.shape[0]
    for l in range(DEPTH):
        mod = (c_act @ w_ada[l] + b_ada[l]).reshape(B, 3, 3, D_MODEL)
        h = ada_norm(x, norm_g[l, 0], mod[:, 0, 0], mod[:, 0, 1])
        x = x + 0.5 * mod[:, 0, 2][:, None, :] * swiglu(h, ffn_w13[l, 0], ffn_w2[l, 0])
        h = ada_norm(x, norm_g[l, 1], mod[:, 1, 0], mod[:, 1, 1])
        x = x + mod[:, 1, 2][:, None, :] * token_mixer(h, w_in[l], fox_fbias[l], cmp_pos[l], cmp_w1[l],
                                                        cmp_w2[l], w_out[l], cos, sin)
        h = ada_norm(x, norm_g[l, 2], mod[:, 2, 0], mod[:, 2, 1])
        x = x + 0.5 * mod[:, 2, 2][:, None, :] * swiglu(h, ffn_w13[l, 1], ffn_w2[l, 1])
    return rms_norm(x, final_g)
```

```python
import numpy as np
from contextlib import ExitStack
import concourse.bass as bass
import concourse.mybir as mybir
from concourse.bass_utils import run_bass_kernel_spmd

F32 = mybir.dt.float32
BF16 = mybir.dt.bfloat16
I32 = mybir.dt.int32
AF = mybir.ActivationFunctionType
ALU = mybir.AluOpType
AX = mybir.AxisListType

D = 1024
T = 2048
NSEQ = 2
DEPTH = 4
DFF = 2816
NJ = 22
INC = 2844
NEG = -30000.0
EPS = 1e-6
FGROUPS = [(0, 6), (6, 6), (12, 5), (17, 5)]
C_NQ, C_KC, C_VC, C_KS, C_VS, C_KW, C_VW, C_NG = 0, 512, 640, 768, 896, 1024, 1152, 1280
C_FQ, C_FK, C_FV, C_FF = 1304, 1560, 1816, 2072
C_MQ, C_MK, C_MV = 2076, 2332, 2588


class Buf:
    __slots__ = ("name", "lw", "rd")

    def __init__(self, name):
        self.name = name
        self.lw = None
        self.rd = {}


class Sched:
    ENG = ("pe", "act", "dve", "pool", "sp")

    def __init__(self, sems):
        self.streams = {e: [] for e in self.ENG}
        self.sem = dict(sems)
        self.cnt = {s: 0 for s in self.sem}
        self.clock = {e: {} for e in self.ENG}
        self.hist = {s: {} for s in self.sem}
        self.nops = 0

    def _needs(self, eng, reads, writes):
        needs = {}

        def add(ev):
            if ev is None:
                return
            s, c = ev
            if needs.get(s, 0) < c:
                needs[s] = c
        for b in reads:
            add(b.lw)
        for b in writes:
            add(b.lw)
            for s, c in b.rd.items():
                add((s, c))
        ck = self.clock[eng]
        waits = []
        for s, c in needs.items():
            if s == eng and eng == "pe":
                continue
            if ck.get(s, 0) < c:
                waits.append((s, c))
        for s, c in waits:
            snap = self.hist[s].get(c)
            if snap:
                for k, v in snap.items():
                    if ck.get(k, 0) < v:
                        ck[k] = v
            ck[s] = c
        return waits

    def op(self, eng, fn, reads=(), writes=()):
        waits = self._needs(eng, reads, writes)
        self.cnt[eng] += 1
        c = self.cnt[eng]
        self.hist[eng][c] = dict(self.clock[eng])
        for b in reads:
            b.rd[eng] = c
        for b in writes:
            b.lw = (eng, c)
            b.rd = {}
        self.streams[eng].append((waits, fn, eng, 1))
        self.nops += 1

    def dma(self, q, fn, sem, reads=(), writes=()):
        waits = self._needs(q, reads, writes)
        if getattr(self, "trace", False) and sem.startswith("ds"):
            print("DMA", q, sem, self.cnt[sem] + 16, "waits", waits, "w", [(b.name, b.lw, dict(b.rd)) for b in writes], "pe cnt", self.cnt["pe"])
        self.cnt[sem] += 16
        c = self.cnt[sem]
        self.hist[sem][c] = dict(self.clock[q])
        for b in reads:
            b.rd[sem] = c
        for b in writes:
            b.lw = (sem, c)
            b.rd = {}
        self.streams[q].append((waits, fn, sem, 16))
        self.nops += 1

    def barrier(self):
        snap = {e: self.cnt[e] for e in self.ENG}
        for e in self.ENG:
            waits = []
            ck = self.clock[e]
            for o, cval in snap.items():
                if o == e and e == "pe":
                    continue
                if cval > 0 and ck.get(o, 0) < cval:
                    waits.append((o, cval))
                    snp = self.hist[o].get(cval)
                    if snp:
                        for k, v in snp.items():
                            if ck.get(k, 0) < v:
                                ck[k] = v
                    ck[o] = cval
            if waits:
                self.streams[e].append((waits, None, None, 0))

    def final_wait(self, eng, bufs):
        waits = self._needs(eng, bufs, ())
        self.streams[eng].append((waits, None, None, 0))

    def replay(self, eng, h):
        for waits, fn, s, inc in self.streams[eng]:
            for ws, wc in waits:
                h.wait_ge(self.sem[ws], wc)
            if fn is not None:
                ins = fn(h)
                ins.then_inc(self.sem[s], inc)


CB_COLS = {}
CF_COLS = {}


def _layout(spec, table):
    off = 0
    for name, n in spec:
        table[name] = (off, n)
        off += n
    return off


CB_N = _layout([("ident", 128), ("ones", 512), ("tri", 512), ("m2", 512), ("perm", 128), ("cmpmask", 2048),
                ("eslc", 2048), ("emoba", 2048), ("ovl", 128), ("sel2", 512), ("selq", 2048), ("mhi", 0)], CB_COLS)
CF_N = _layout([("ident", 128), ("impvalid", 512), ("impbias", 512), ("mobabias", 128), ("invf", 1), ("sgn", 1),
                ("mhi", 1), ("mlo", 1), ("ones", 128)], CF_COLS)


def make_consts():
    cb = np.zeros((128, CB_N), np.float32)
    cf = np.zeros((128, CF_N), np.float32)

    def setb(name, arr):
        o, n = CB_COLS[name]
        cb[:arr.shape[0], o:o + arr.shape[1]] = arr

    def setf(name, arr):
        o, n = CF_COLS[name]
        cf[:arr.shape[0], o:o + arr.shape[1]] = arr
    setb("ident", np.eye(128, dtype=np.float32))
    setb("ones", np.ones((128, 512), np.float32))
    k = np.arange(128)[:, None]
    c = np.arange(512)[None, :]
    setb("tri", np.where(k <= c, 0.0, NEG).astype(np.float32))
    setb("m2", np.where((c < 384) | (k > c - 384), 0.0, NEG).astype(np.float32))
    perm = np.zeros((128, 128), np.float32)
    for m in range(128):
        d = m % 64
        if d < 8:
            perm[m + 8, m] = 1.0
        elif d < 16:
            perm[m - 8, m] = 1.0
    setb("perm", perm)
    cc = np.arange(127)[:, None]
    t = np.arange(2048)[None, :]
    setb("cmpmask", np.where(cc * 16 + 31 <= t, 0.0, NEG).astype(np.float32))
    es = np.zeros((32, 16, 128), np.float32)
    for kt in range(16):
        for kk in range(128):
            es[(kt * 128 + kk) // 64, kt, kk] = 1.0
    setb("eslc", es.reshape(32, 2048))
    em = np.zeros((8, 16, 128), np.float32)
    for kt in range(16):
        em[kt // 2, kt, :] = 1.0
    setb("emoba", em.reshape(8, 2048))
    ov = np.zeros((127, 128), np.float32)
    c0 = np.arange(127)[:, None] * 16
    j0 = np.arange(32)[None, :] * 64
    ov[:, 64] = 1.0
    ov[:, 65:97] = ((c0 < j0 + 64) & (c0 + 32 > j0)).astype(np.float32)
    setb("ovl", ov)
    s2 = np.zeros((8, 4, 128), np.float32)
    sq = np.zeros((8, 4, 512), np.float32)
    for h in range(4):
        s2[h, h, :] = 1.0
        s2[4 + h, h, :] = 1.0
        sq[h, h, :] = 1.0
        sq[4 + h, h, :] = 1.0
    setb("sel2", s2.reshape(8, 512))
    setb("selq", sq.reshape(8, 2048))
    setf("ident", np.eye(128, dtype=np.float32))
    setf("ones", np.ones((128, 128), np.float32))
    q = np.arange(2048)
    tb = q // 64
    jj = np.arange(32)[None, :]
    valid = jj <= tb[:, None]
    forced = (jj == 0) | (jj == tb[:, None]) | (jj == tb[:, None] - 1)
    iv = (valid & ~forced).astype(np.float32)
    ib = np.where(~valid, -1e9, np.where(forced, 1e9, 0.0)).astype(np.float32)
    setf("impvalid", iv.reshape(16, 128, 32).transpose(1, 0, 2).reshape(128, 512))
    setf("impbias", ib.reshape(16, 128, 32).transpose(1, 0, 2).reshape(128, 512))
    nb = np.arange(8)[None, :]
    own = (q // 256)[:, None]
    mb = np.where(nb == own, 1e9, np.where(nb < own, 0.0, -1e9)).astype(np.float32)
    setf("mobabias", mb.reshape(16, 128, 8).transpose(1, 0, 2).reshape(128, 128))
    inv = (500000.0 ** (-np.arange(0, 16, 2, dtype=np.float32) / 16)).astype(np.float32)
    invf = np.zeros((128, 1), np.float32)
    sgn = np.zeros((128, 1), np.float32)
    for p in range(128):
        d = p % 64
        if d < 16:
            invf[p] = inv[d % 8]
            sgn[p] = -1.0 if d < 8 else 1.0
    setf("invf", invf)
    setf("sgn", sgn)
    mh = np.zeros((128, 1), np.float32)
    mh[0:4] = 1.0
    ml = np.zeros((128, 1), np.float32)
    ml[4:8] = 1.0
    setf("mhi", mh)
    setf("mlo", ml)
    return cb, cf


def build_program(layers, final_norm=True, nseq=NSEQ, stop=None, skip_mods=False):
    nc = bass.Bass("TRN2", target_bir_lowering=False)
    L = len(layers)

    def dt_in(name, shape, dt=F32):
        return nc.dram_tensor(name, list(shape), dt, kind="ExternalInput").ap()
    x_d = dt_in("x", [nseq, T, D])
    pos_d = dt_in("pos", [nseq, T], I32)
    prow_d = dt_in("prow", [512, 128])
    fb8_d = dt_in("fb8", [8, DEPTH])
    wada_d = dt_in("w_ada", [DEPTH, D, 9 * D])
    win_d = dt_in("w_in", [DEPTH, D, INC])
    cpos_d = dt_in("cmp_pos", [DEPTH, 2, 32, 64])
    cw1_d = dt_in("cmp_w1", [DEPTH, 2, 2048, 256])
    cw2_d = dt_in("cmp_w2", [DEPTH, 2, 256, 64])
    wout_d = dt_in("w_out", [DEPTH, D, D])
    w13_d = dt_in("ffn_w13", [DEPTH, 2, D, 2 * DFF])
    w2_d = dt_in("ffn_w2", [DEPTH, 2, DFF, D])
    cb_d = dt_in("cb", [128, CB_N])
    cf_d = dt_in("cf", [128, CF_N])
    y_d = nc.dram_tensor("y", [nseq, T, D], F32, kind="ExternalOutput").ap()

    es = ExitStack()
    ARENA = 53200
    arena = es.enter_context(nc.sbuf_tensor("arena", [128, ARENA], F32))
    PS = [es.enter_context(nc.psum_tensor("ps%d" % i, [128, 512], F32)) for i in range(8)]
    BPS = [Buf("ps%d" % i) for i in range(8)]
    NSLOT = 3
    semnames = list(Sched.ENG) + ["dx0", "dx1", "dc", "dm0", "dm1", "dcf", "dfb", "dy0", "dy1", "dada0", "dada1", "dw2a", "dw2b", "dq0", "dq1", "dq2", "dq3", "dq4"] + ["ds%d" % i for i in range(NSLOT)]
    sems = {n: es.enter_context(nc.semaphore(n)) for n in semnames}
    S = Sched(sems)

    state = {"off": 0}

    def alloc(nbytes):
        o = state["off"]
        state["off"] += (nbytes + 3) // 4
        assert state["off"] <= ARENA, ("SBUF overflow", state["off"] * 4)
        return o

    def vf(o, n, p0=0, p1=128):
        return arena[p0:p1, o:o + n]

    def vb(o, n, p0=0, p1=128):
        return arena[p0:p1, o:o + n // 2].bitcast(BF16)

    o_xT = alloc(8 * T * 4)
    o_hT = alloc(8 * T * 2)
    o_cb = alloc(CB_N * 2)
    o_cf = alloc(CF_N * 4)
    o_pt = alloc(512 * 4)
    o_modp = alloc(L * 3 * 3 * nseq * 8 * 4)
    o_ropeC = alloc(T * 2)
    o_ropeS = alloc(T * 2)
    o_sq = [alloc(512 * 2) for _ in range(2)]
    o_rstd = alloc(512 * 4)
    o_tmp = [alloc(512 * 4) for _ in range(2)]
    o_ring = [alloc(4096) for _ in range(NSLOT)]
    o_region = state["off"]

    def xT(c, t):
        return vf(o_xT + c * T + t * 512, 512)

    def hT(c, t0, n):
        return arena[:, o_hT + (c * T + t0) // 2:o_hT + (c * T + t0 + n) // 2].bitcast(BF16)

    def cbv(name, c0=0, n=None, p0=0, p1=128):
        o, nn = CB_COLS[name]
        if n is None:
            n = nn - c0
        a = o + c0
        assert a % 2 == 0 and n % 2 == 0, (name, c0, n)
        return arena[p0:p1, o_cb + a // 2:o_cb + (a + n) // 2].bitcast(BF16)

    def cfv(name, c0=0, n=None, p0=0, p1=128):
        o, nn = CF_COLS[name]
        if n is None:
            n = nn - c0
        return arena[p0:p1, o_cf + o + c0:o_cf + o + c0 + n]

    Bx = {}

    def BX(c, t):
        return Bx.setdefault((c, t), Buf("x%d_%d" % (c, t)))
    Bh = {}

    def BH(c, t):
        return Bh.setdefault((c, t), Buf("h%d_%d" % (c, t)))
    Bcb, Bcf, Bpt, Bmodp, Brope = Buf("cb"), Buf("cf"), Buf("pt"), Buf("modp"), Buf("rope")
    Bsq = [Buf("sq0"), Buf("sq1")]
    Brstd = Buf("rstd")
    Btmp = [Buf("tmp0"), Buf("tmp1")]
    Bring = [Buf("ring%d" % i) for i in range(NSLOT)]
    Bada = None
    By = [Buf("y0"), Buf("y1")]
    ring_state = {"i": 0}

    def ACT(fn, r, w):
        S.op("act", fn, r, w)

    def DVE(fn, r, w):
        S.op("dve", fn, r, w)

    def PE(fn, r, w):
        S.op("pe", fn, r, w)

    def POOL(fn, r, w):
        S.op("pool", fn, r, w)

    def ring_load(dmas):
        i = ring_state["i"]
        ring_state["i"] = (i + 1) % NSLOT
        for dst_fn, src in dmas:
            dst = dst_fn(o_ring[i])
            S.dma("pool", (lambda d, s: (lambda e: e.dma_start(out=d, in_=s)))(dst, src), "ds%d" % i, (), [Bring[i]])
        return i

    def ringv(i, n, c0=0, p0=0, p1=128):
        return arena[p0:p1, o_ring[i] + c0 // 2:o_ring[i] + (c0 + n) // 2].bitcast(BF16)

    region_bufs = {}

    def RB(name):
        b = region_bufs.get(name)
        if b is None:
            b = Buf(name)
            b.rd = dict(fence["rd"])
            region_bufs[name] = b
        return b
    fence = {"rd": {}}

    def region_fence():
        acc = dict(fence["rd"])
        for b in region_bufs.values():
            for ev in ([b.lw] if b.lw else []) + list(b.rd.items()):
                s_, c_ = ev
                if acc.get(s_, 0) < c_:
                    acc[s_] = c_
        fence["rd"] = acc
        region_bufs.clear()


    for a in range(0, CB_N, 1024):
        b = min(CB_N, a + 1024)
        S.dma("pool", (lambda a, b: (lambda e: e.dma_start(out=vb(o_cb, CB_N)[:, a:b], in_=cb_d[:, a:b])))(a, b), "dc", (), [Bcb])
    S.dma("sp", lambda e: e.dma_start(out=vf(o_cf, CF_N), in_=cf_d[:, :]), "dcf", (), [Bcf])
    ident_f = cfv("ident")
    ident_b = cbv("ident")
    for blk in range(4):
        stg = vf(o_tmp[blk % 2], 128)
        S.dma("sp", (lambda blk, stg: (lambda e: e.dma_start(out=stg, in_=prow_d[blk * 128:(blk + 1) * 128, :])))(blk, stg),
              "dm%d" % (blk % 2), (), [Btmp[blk % 2]])
        PE((lambda stg: (lambda e: e.transpose(PS[6][:, 0:128], stg, ident_f)))(stg), [Btmp[blk % 2], Bcf], [BPS[6]])
        DVE((lambda blk: (lambda e: e.tensor_copy(out=vf(o_pt + blk * 128, 128), in_=PS[6][:, 0:128])))(blk), [BPS[6]], [Bpt])
    R_BADA, R_NG, R_FG, R_C = 0, 288, 384, 392

    def ptc(r0, n, step=1):
        if step == 1:
            return arena[:, o_pt + r0:o_pt + r0 + n]
        return arena[:, o_pt + r0:o_pt + r0 + (n - 1) * step + 1:step]

    o_fb = alloc(DEPTH * 4)
    Bfb = Buf("fb")
    S.dma("sp", lambda e: e.dma_start(out=vf(o_fb, DEPTH, 0, 8), in_=fb8_d[:, :]), "dfb", (), [Bfb])
    DVE(lambda e: e.tensor_scalar(out=vf(o_fb, DEPTH, 0, 8), in0=vf(o_fb, DEPTH, 0, 8), scalar1=-1.0, scalar2=None, op0=ALU.mult), [Bfb], [Bfb])
    o_region = state["off"]

    def modv(li, sub, kind, s):
        o = o_modp + (((li * 3 + sub) * 3 + kind) * nseq + s) * 8
        return arena[:, o:o + 8]

    o_cact = alloc(16 * 4)
    Bcact = Buf("cact")
    o_modraw = alloc(72 * nseq * 4)
    o_eps = alloc(4)
    state["off"] = (state["off"] + 15) // 16 * 16
    o_region = state["off"]
    o_ada = [o_region + i * 2048 for i in range(2)]
    assert o_region + 4096 <= ARENA
    ACT(lambda e: e.activation(out=vf(o_cact, 8 * nseq), in_=ptc(R_C, 8 * nseq), func=AF.Silu), [Bpt], [Bcact])
    Bmodraw = Buf("modraw")
    Bada = [RB("ada0"), RB("ada1")]
    for li, l in enumerate([] if skip_mods else layers):
        for slab in range(36):
            bi = slab % 2
            S.dma("sp", (lambda l, slab, bi: (lambda e: e.dma_start(
                out=vf(o_ada[bi], 2048).rearrange("p (k n) -> p k n", k=8),
                in_=wada_d[l, :, slab * 256:(slab + 1) * 256].rearrange("(k p) n -> p k n", p=128))))(l, slab, bi),
                "dada%d" % bi, (), [Bada[bi]])

            def mm(e, bi=bi, slab=slab):
                ins = None
                for n2 in range(2):
                    n = slab * 2 + n2
                    for k in range(8):
                        ins = e.matmul(PS[6][:, n * nseq:(n + 1) * nseq],
                                       lhsT=arena[:, o_ada[bi] + k * 256 + n2 * 128:o_ada[bi] + k * 256 + n2 * 128 + 128],
                                       rhs=arena[:, o_cact + k:o_cact + k + 8 * (nseq - 1) + 1:8],
                                       start=(k == 0), stop=(k == 7))
                return ins
            PE(mm, [Bada[bi], Bcact], [BPS[6]])
        for s in range(nseq):
            DVE((lambda li, l, s: (lambda e: e.tensor_tensor(
                out=arena[:, o_modraw + s:o_modraw + s + 71 * nseq + 1:nseq],
                in0=PS[6][:, s:s + 71 * nseq + 1:nseq], in1=ptc(R_BADA + l * 72, 72), op=ALU.add)))(li, l, s), [BPS[6], Bpt], [Bmodraw])
        for sub in range(3):
            for s in range(nseq):
                def raw(kind, sub=sub, s=s):
                    a = o_modraw + (sub * 24 + kind * 8) * nseq + s
                    return arena[:, a:a + 7 * nseq + 1:nseq]
                DVE((lambda li, sub, s, raw: (lambda e: e.tensor_copy(out=modv(li, sub, 0, s), in_=raw(0))))(li, sub, s, raw), [Bmodraw], [Bmodp])
                DVE((lambda li, l, sub, s, raw: (lambda e: e.scalar_tensor_tensor(
                    out=modv(li, sub, 1, s), in0=raw(1), scalar=1.0, in1=ptc(R_NG + (l * 3 + sub) * 8, 8),
                    op0=ALU.add, op1=ALU.mult)))(li, l, sub, s, raw), [Bmodraw, Bpt], [Bmodp])
                gs = 1.0 if sub == 1 else 0.5
                DVE((lambda li, sub, s, raw, gs: (lambda e: e.tensor_scalar(
                    out=modv(li, sub, 2, s), in0=raw(2), scalar1=gs, scalar2=None, op0=ALU.mult)))(li, sub, s, raw, gs), [Bmodraw], [Bmodp])
    o_region = state["off"]

    def norm_mod(li, sub, s, t, final=False, out_fn=None):
        for c in range(8):
            i = c % 2
            ACT((lambda c, i: (lambda e: e.activation(out=vb(o_sq[i], 512), in_=xT(c, t), func=AF.Square)))(c, i), [BX(c, t)], [Bsq[i]])
            PE((lambda c, i: (lambda e: e.matmul(PS[7][:, :], lhsT=cbv("ones", 0, 128), rhs=vb(o_sq[i], 512), start=(c == 0), stop=(c == 7))))(c, i),
               [Bsq[i], Bcb], [BPS[7]])
        ACT(lambda e: e.activation(out=vf(o_rstd, 512), in_=PS[7][:, :], func=AF.Sqrt, bias=EPS_AP, scale=1.0 / D), [BPS[7], Beps], [Brstd])
        DVE(lambda e: e.reciprocal(out=vf(o_rstd, 512), in_=vf(o_rstd, 512)), [Brstd], [Brstd])
        for c in range(8):
            i = c % 2
            if final:
                DVE((lambda c, i: (lambda e: e.scalar_tensor_tensor(out=xT(c, t), in0=xT(c, t), scalar=ptc(R_FG + c, 1), in1=vf(o_rstd, 512),
                                                                    op0=ALU.mult, op1=ALU.mult)))(c, i), [BX(c, t), Brstd, Bpt], [BX(c, t)])
            else:
                DVE((lambda c, i: (lambda e: e.scalar_tensor_tensor(out=vf(o_tmp[i], 512), in0=xT(c, t), scalar=modv(li, sub, 1, s)[:, c:c + 1],
                                                                    in1=vf(o_rstd, 512), op0=ALU.mult, op1=ALU.mult)))(c, i),
                    [BX(c, t), Brstd, Bmodp], [Btmp[i]])
                ACT((lambda c, i: (lambda e: e.activation(out=hT(c, t * 512, 512), in_=vf(o_tmp[i], 512), func=AF.Identity,
                                                          bias=modv(li, sub, 0, s)[:, c:c + 1], scale=1.0)))(c, i),
                    [Btmp[i], Bmodp], [BH(c, t)])

    Beps = Buf("eps")
    EPS_AP = arena[:, o_eps:o_eps + 1]
    DVE(lambda e: e.memset(EPS_AP, EPS), [], [Beps])
    o_region = state["off"]

    def ffn(li, l, f, s):
        sub = 0 if f == 0 else 2
        st = {"off": o_region}

        def ra(nbytes):
            o = st["off"]
            st["off"] += (nbytes + 3) // 4
            assert st["off"] <= ARENA, ("SBUF overflow ffn", st["off"] * 4)
            return o
        o_g = ra(6 * T * 2)
        o_w2 = [ra(6 * 1024 * 2) for _ in range(2)]
        o_sa = [ra(512 * 2) for _ in range(2)]
        Bg = {}
        Bw2 = [RB("ffn_w2_0"), RB("ffn_w2_1")]
        Bsa = [RB("ffn_sa0"), RB("ffn_sa1")]

        def BG(jj, t):
            return Bg.setdefault((jj, t), RB("ffn_g%d_%d" % (jj, t)))
        for t in range(4):
            norm_mod(li, sub, s, t)
        cnt = 0
        pending = []

        def issue_w13(j):
            return ring_load([
                (lambda o: arena[:, o:o + 1024].bitcast(BF16).rearrange("p (k n) -> p k n", k=8)[:, :, 0:128],
                 w13_d[l, f, :, j * 128:(j + 1) * 128].rearrange("(k p) n -> p k n", p=128)),
                (lambda o: arena[:, o:o + 1024].bitcast(BF16).rearrange("p (k n) -> p k n", k=8)[:, :, 128:256],
                 w13_d[l, f, :, DFF + j * 128:DFF + (j + 1) * 128].rearrange("(k p) n -> p k n", p=128))])
        jobs = list(range(NJ))
        slots = {}
        for j in jobs[:NSLOT - 1]:
            slots[j] = issue_w13(j)
        for gi, (j0, jg) in enumerate(FGROUPS):
            wb = gi % 2
            S.dma("pool", (lambda j0, jg, wb: (lambda e: e.dma_start(
                out=arena[:, o_w2[wb]:o_w2[wb] + jg * 512].bitcast(BF16).rearrange("p (j n) -> p j n", j=jg),
                in_=w2_d[l, f, j0 * 128:(j0 + jg) * 128, :].rearrange("(j p) n -> p j n", p=128))))(j0, jg, wb),
                "dw2a" if wb == 0 else "dw2b", (), [Bw2[wb]])
            for jj in range(jg):
                j = j0 + jj
                nxt = j + NSLOT - 1
                if nxt < NJ:
                    slots[nxt] = issue_w13(nxt)
                sl = slots[j]
                for t in range(4):
                    pa, pb = PS[2 * (cnt % 2)], PS[2 * (cnt % 2) + 1]
                    Ba, Bb = BPS[2 * (cnt % 2)], BPS[2 * (cnt % 2) + 1]
                    si = cnt % 2
                    cnt += 1

                    def mm(e, sl=sl, t=t, pa=pa, pb=pb):
                        ins = None
                        for (ps, c0) in ((pa, 0), (pb, 128)):
                            for k in range(8):
                                ins = e.matmul(ps[:, :], lhsT=ringv(sl, 128, k * 256 + c0), rhs=hT(k, t * 512, 512), start=(k == 0), stop=(k == 7))
                        return ins
                    PE(mm, [Bring[sl]] + [BH(k, t) for k in range(8)], [Ba, Bb])
                    ACT((lambda pa, si: (lambda e: e.activation(out=vb(o_sa[si], 512), in_=pa[:, :], func=AF.Silu)))(pa, si), [Ba], [Bsa[si]])
                    DVE((lambda pb, si, jj, t: (lambda e: e.tensor_tensor(
                        out=arena[:, o_g + (jj * T + t * 512) // 2:o_g + (jj * T + t * 512 + 512) // 2].bitcast(BF16),
                        in0=pb[:, :], in1=vb(o_sa[si], 512), op=ALU.mult)))(pb, si, jj, t), [Bb, Bsa[si]], [BG(jj, t)])
            for t in range(4):
                for n in range(8):
                    po = PS[4 + (cnt % 2)]
                    Bo = BPS[4 + (cnt % 2)]
                    cnt += 1

                    def mm2(e, wb=wb, jg=jg, n=n, t=t, po=po):
                        ins = None
                        for jj in range(jg):
                            a = o_w2[wb] + (jj * 1024 + n * 128) // 2
                            ins = e.matmul(po[:, :], lhsT=arena[:, a:a + 64].bitcast(BF16),
                                           rhs=arena[:, o_g + (jj * T + t * 512) // 2:o_g + (jj * T + t * 512 + 512) // 2].bitcast(BF16),
                                           start=(jj == 0), stop=(jj == jg - 1))
                        return ins
                    PE(mm2, [Bw2[wb]] + [BG(jj, t) for jj in range(jg)], [Bo])
                    DVE((lambda n, t, po: (lambda e: e.scalar_tensor_tensor(out=xT(n, t), in0=po[:, :], scalar=modv(li, sub, 2, s)[:, n:n + 1],
                                                                            in1=xT(n, t), op0=ALU.mult, op1=ALU.add)))(n, t, po),
                        [Bo, BX(n, t), Bmodp], [BX(n, t)])

    def load_x(s):
        for tt in range(16):
            i = tt % 2
            stg = arena[:, o_region + i * 1024:o_region + (i + 1) * 1024]
            Bst = RB("xstg%d" % i)
            import os as _os
            S.dma("pool" if "xpool" in _os.environ.get("KDBG", "") else "sp", (lambda tt, stg: (lambda e: e.dma_start(out=stg, in_=x_d[s, tt * 128:(tt + 1) * 128, :])))(tt, stg), "dx%d" % i, (), [Bst])
            if "ldma" in _os.environ.get("KDBG", ""):
                DVE((lambda tt, stg: (lambda e: e.tensor_copy(out=xT(tt % 8, tt // 8)[:, 0:512], in_=stg[:, 0:512])))(tt, stg), [Bst], [BX(tt % 8, tt // 8)])
                continue
            for half in range(2):
                pb = PS[half + 2 * i]

                def tr(e, stg=stg, half=half, pb=pb):
                    ins = None
                    for cc in range(4):
                        c = half * 4 + cc
                        ins = e.transpose(pb[:, cc * 128:(cc + 1) * 128], stg[:, c * 128:(c + 1) * 128], ident_f)
                    return ins
                PE(tr, [Bst, Bcf], [BPS[half + 2 * i]])
                t, q = tt // 4, tt % 4
                dst = arena[:, o_xT:o_xT + 8 * T].rearrange("p (c n) -> p c n", c=8)[:, half * 4:(half + 1) * 4, t * 512 + q * 128:t * 512 + (q + 1) * 128]
                src = pb[:, :].rearrange("p (c n) -> p c n", c=4)
                wb_ = [BX(half * 4 + cc, t) for cc in range(4)]
                if half == 0:
                    ACT((lambda dst, src: (lambda e: e.activation(out=dst, in_=src, func=AF.Identity)))(dst, src), [BPS[half + 2 * i]], wb_)
                else:
                    DVE((lambda dst, src: (lambda e: e.tensor_copy(out=dst, in_=src)))(dst, src), [BPS[half + 2 * i]], wb_)

    def store_x(s, normalize):
        if normalize:
            for t in range(4):
                norm_mod(0, 0, s, t, final=True)
        for tt in range(16):
            i = tt % 2
            stg = arena[:, o_region + i * 1024:o_region + (i + 1) * 1024]
            Bst = RB("ystg%d" % i)
            t, q = tt // 4, tt % 4
            for half in range(2):
                pb = PS[half + 2 * i]

                def tr(e, half=half, pb=pb, t=t, q=q):
                    ins = None
                    for cc in range(4):
                        c = half * 4 + cc
                        ins = e.transpose(pb[:, cc * 128:(cc + 1) * 128], xT(c, t)[:, q * 128:(q + 1) * 128], ident_f)
                    return ins
                PE(tr, [BX(half * 4 + cc, t) for cc in range(4)] + [Bcf], [BPS[half + 2 * i]])
                if half == 0:
                    ACT((lambda stg, pb: (lambda e: e.activation(out=stg[:, 0:512], in_=pb[:, :], func=AF.Identity)))(stg, pb), [BPS[half + 2 * i]], [Bst])
                else:
                    DVE((lambda stg, pb: (lambda e: e.tensor_copy(out=stg[:, 512:1024], in_=pb[:, :])))(stg, pb), [BPS[half + 2 * i]], [Bst])
            S.dma("sp", (lambda tt, stg: (lambda e: e.dma_start(out=y_d[s, tt * 128:(tt + 1) * 128, :], in_=stg)))(tt, stg), "dy%d" % i, [Bst], [By[i]])

    DBG.update(o_ring=o_ring, o_fb=o_fb, o_cact=o_cact, o_modraw=o_modraw, o_eps=o_eps, o_region=o_region, o_tmp=o_tmp, o_rstd=o_rstd, o_sq=o_sq, o_modp=o_modp, o_pt=o_pt, o_cb=o_cb, o_cf=o_cf, o_ropeC=o_ropeC)
    ctx = dict(nc=nc, S=S, arena=arena, PS=PS, BPS=BPS, alloc_region=lambda: o_region, RB=RB, region_fence=region_fence,
               ACT=ACT, DVE=DVE, PE=PE, POOL=POOL, ring_load=ring_load, ringv=ringv, Bring=Bring, cbv=cbv, cfv=cfv, Bcb=Bcb, Bcf=Bcf,
               xT=xT, hT=hT, BX=BX, BH=BH, modv=modv, Bmodp=Bmodp, norm_mod=norm_mod, ARENA=ARENA,
               o_ropeC=o_ropeC, o_ropeS=o_ropeS, Brope=Brope, o_tmp=o_tmp, Btmp=Btmp, o_fb=o_fb, Bfb=Bfb,
               win_d=win_d, cpos_d=cpos_d, cw1_d=cw1_d, cw2_d=cw2_d, wout_d=wout_d, pos_d=pos_d, ident_f=ident_f, ident_b=ident_b)

    import os as _os
    dbg = _os.environ.get("KDBG", "")
    for s in range(nseq):
        region_fence()
        if "noload" in dbg:
            for c in range(8):
                for t in range(4):
                    DVE((lambda c, t: (lambda e: e.memset(xT(c, t), 1.0)))(c, t), [], [BX(c, t)])
        else:
            load_x(s)
        if stop != "load":
            region_fence()
            rope_tables(ctx, s)
            for li, l in enumerate(layers):
                region_fence()
                ffn(li, l, 0, s)
                if stop == "ffn0":
                    break
                region_fence()
                mixer(ctx, li, l, s, stop)
                if stop in ("mix", "nsa", "fox"):
                    break
                region_fence()
                ffn(li, l, 1, s)
        region_fence()
        store_x(s, final_norm and stop is None)
    S.final_wait("sp", By)
    with nc.Block() as block:
        @block.tensor
        def _(e):
            S.replay("pe", e)

        @block.scalar
        def _(e):
            S.replay("act", e)

        @block.vector
        def _(e):
            S.replay("dve", e)

        @block.gpsimd
        def _(e):
            S.replay("pool", e)

        @block.sync
        def _(e):
            S.replay("sp", e)
    es.close()
    return nc


DBG = {}


import os as _os2


class NS:
    def __init__(self, d):
        self.__dict__.update(d)


TWO_PI_HI = 6.28125
TWO_PI_LO = 0.0019353071795864769
PI_CL = 3.1415925


def rope_tables(ctx, s):
    c = NS(ctx)
    A = c.arena
    o = c.alloc_region()
    o_pi, o_ang, o_u, o_ui = o, o + 512, o + 1024, o + 1536
    for t in range(4):
        Bpi, Bang, Bu, Bui = c.RB("r_pi"), c.RB("r_ang"), c.RB("r_u"), c.RB("r_ui")
        pi_i = A[:, o_pi:o_pi + 512].bitcast(I32)
        ang = A[:, o_ang:o_ang + 512]
        u = A[:, o_u:o_u + 512]
        ui = A[:, o_ui:o_ui + 512].bitcast(I32)
        c.S.dma("sp", (lambda t, pi_i: (lambda e: e.dma_start(out=pi_i, in_=c.pos_d[s:s + 1, t * 512:(t + 1) * 512].to_broadcast([128, 512]))))(t, pi_i),
                "dq0", (), [Bpi])
        c.DVE((lambda pi_i, ang: (lambda e: e.tensor_copy(out=ang, in_=pi_i)))(pi_i, ang), [Bpi], [Bang])
        c.DVE((lambda ang: (lambda e: e.tensor_scalar(out=ang, in0=ang, scalar1=c.cfv("invf"), scalar2=None, op0=ALU.mult)))(ang), [Bang, c.Bcf], [Bang])
        for which, off in (("S", 0.0), ("C", float(np.pi / 2))):
            c.DVE((lambda ang, u, off: (lambda e: e.tensor_scalar(out=u, in0=ang, scalar1=off, scalar2=float(1.0 / (2 * np.pi)), op0=ALU.add, op1=ALU.mult)))(ang, u, off), [Bang], [Bu])
            c.DVE((lambda u, ui: (lambda e: e.tensor_copy(out=ui, in_=u)))(u, ui), [Bu], [Bui])
            c.DVE((lambda u, ui: (lambda e: e.tensor_copy(out=u, in_=ui)))(u, ui), [Bui], [Bu])
            c.DVE((lambda ang, u, ui: (lambda e: e.scalar_tensor_tensor(out=ui.bitcast(F32), in0=u, scalar=-TWO_PI_HI, in1=ang, op0=ALU.mult, op1=ALU.add)))(ang, u, ui), [Bu, Bang], [Bui])
            c.DVE((lambda u, ui: (lambda e: e.scalar_tensor_tensor(out=u, in0=u, scalar=-TWO_PI_LO, in1=ui.bitcast(F32), op0=ALU.mult, op1=ALU.add)))(u, ui), [Bu, Bui], [Bu])
            c.DVE((lambda u, off: (lambda e: e.tensor_scalar(out=u, in0=u, scalar1=off, scalar2=PI_CL, op0=ALU.add, op1=ALU.min)))(u, off), [Bu], [Bu])
            c.DVE((lambda u: (lambda e: e.tensor_scalar(out=u, in0=u, scalar1=-PI_CL, scalar2=None, op0=ALU.max)))(u), [Bu], [Bu])
            if which == "S":
                dst = A[:, c.o_ropeS + t * 256:c.o_ropeS + (t + 1) * 256].bitcast(BF16)
                c.ACT((lambda u, dst: (lambda e: e.activation(out=dst, in_=u, func=AF.Sin, scale=c.cfv("sgn"))))(u, dst), [Bu, c.Bcf], [c.Brope])
            else:
                dst = A[:, c.o_ropeC + t * 256:c.o_ropeC + (t + 1) * 256].bitcast(BF16)
                c.ACT((lambda u, dst: (lambda e: e.activation(out=dst, in_=u, func=AF.Sin)))(u, dst), [Bu], [c.Brope])


def causal_tiles(t):
    tl = [(kt, 0, 512, "full") for kt in range(4 * t)]
    for i in range(4):
        tl.append((4 * t + i, 128 * i, 512, "diag"))
    return tl


def window_tiles(t):
    tl = []
    if t > 0:
        tl.append((4 * t - 1, 0, 512, "low3"))
        for i in range(3):
            tl.append((4 * t - 4 + i, 0, 128 * (i + 1), "low%d" % i))
    for i in range(4):
        tl.append((4 * t + i, 128 * i, 512, "diag"))
    if t == 0:
        pass
    return tl


def mixer(ctx, li, l, s, stop):
    c = NS(ctx)
    A, S, PS, BPS, RB = c.arena, c.S, c.PS, c.BPS, c.RB
    ACT, DVE, PE = c.ACT, c.DVE, c.PE
    for t in range(4):
        c.norm_mod(li, 1, s, t)
    HB = [c.BH(k, t) for k in range(8) for t in range(4)]
    for _i in range(int(_os2.environ.get("RINGSHIFT", "0"))):
        c.ring_load([(lambda o: A[:, o:o + 1024].bitcast(BF16).rearrange("p (k n) -> p k n", k=8), c.win_d[l, :, 0:256].rearrange("(k p) n -> p k n", p=128))])
    cnt = {"s": 0, "o": 0, "w": 0, "p": 0}
    ropeC = lambda t0, n: A[:, c.o_ropeC + t0 // 2:c.o_ropeC + (t0 + n) // 2].bitcast(BF16)
    ropeS = lambda t0, n: A[:, c.o_ropeS + t0 // 2:c.o_ropeS + (t0 + n) // 2].bitcast(BF16)

    def slotdst(c0, n, p0=0, p1=128, k=8, w=256):
        return lambda o: A[p0:p1, o:o + 1024].bitcast(BF16).rearrange("p (k n) -> p k n", k=k)[:, :, c0:c0 + n]

    def win_src(col, n):
        return c.win_d[l, :, col:col + n].rearrange("(k p) n -> p k n", p=128)

    def make_region():
        c.region_fence()
        if "bar" in _os2.environ.get("KDBG", "").split(","):
            S.barrier()
        st = {"off": c.alloc_region()}

        def ra(nbytes):
            o = st["off"]
            st["off"] += (nbytes + 3) // 4
            assert st["off"] <= c.ARENA, ("SBUF overflow mixer", st["off"] * 4)
            return o
        return ra

    def bfv(o, n, p0=0, p1=128, c0=0):
        a = c0 // 2
        b = (c0 + n + 1) // 2
        v = A[p0:p1, o + a:o + b].bitcast(BF16)
        if c0 % 2 == 0 and n % 2 == 0:
            return v
        return v[:, c0 % 2:c0 % 2 + n]

    def proj_fm(slot, c0, M, t, bank):
        def mm(e):
            ins = None
            for k in range(8):
                ins = e.matmul(PS[bank][0:M, :], lhsT=c.ringv(slot, M, k * 256 + c0), rhs=c.hT(k, t * 512, 512), start=(k == 0), stop=(k == 7))
            return ins
        PE(mm, [c.Bring[slot]] + [c.BH(k, t) for k in range(8)], [BPS[bank]])

    def proj_tm(slot, c0, n, tt, bank):
        def mm(e):
            ins = None
            for k in range(8):
                ins = e.matmul(PS[bank][:, 0:n], lhsT=c.hT(k, tt * 128, 128), rhs=c.ringv(slot, n, k * 256 + c0), start=(k == 0), stop=(k == 7))
            return ins
        PE(mm, [c.Bring[slot]] + [c.BH(k, tt // 4) for k in range(8)], [BPS[bank]])

    def rope_evac(bank, scale, dst, Bdst, t, o_rb, Brb):
        i = cnt["p"] % 2
        cnt["p"] += 1
        qb = bfv(o_rb[i], 512)
        xb = 4 + i
        ACT(lambda e: e.activation(out=qb, in_=PS[bank][:, :], func=AF.Identity, scale=scale), [BPS[bank]], [Brb[i]])
        PE(lambda e: e.matmul(PS[xb][:, :], lhsT=c.cbv("perm"), rhs=qb, start=True, stop=True), [Brb[i], c.Bcb], [BPS[xb]])
        t1 = A[:, c.o_tmp[0]:c.o_tmp[0] + 512]
        t2 = A[:, c.o_tmp[1]:c.o_tmp[1] + 512]
        DVE(lambda e: e.tensor_tensor(out=t1, in0=PS[xb][:, :], in1=ropeS(t * 512, 512), op=ALU.mult), [BPS[xb], c.Brope], [c.Btmp[0]])
        DVE(lambda e: e.tensor_tensor(out=t2, in0=qb, in1=ropeC(t * 512, 512), op=ALU.mult), [Brb[i], c.Brope], [c.Btmp[1]])
        DVE(lambda e: e.tensor_tensor(out=dst, in0=t1, in1=t2, op=ALU.add), [c.Btmp[0], c.Btmp[1]], [Bdst])

    def plain_evac(bank, M, scale, dst, Bdst):
        ACT(lambda e: e.activation(out=dst, in_=PS[bank][0:M, :], func=AF.Identity, scale=scale), [BPS[bank]], [Bdst])

    def next_bank():
        b = cnt["p"] % 4
        return b

    def attend(t, tiles, qk_fn, extra_fn, W, vrhs_fn, vbufs, o_pt, Bpt, kparts=128):
        ob = 4 + (cnt["o"] % 2)
        cnt["o"] += 1
        n = len(tiles)
        first = {}
        last = {}
        for i, (kt, lo, hi, kind) in enumerate(tiles):
            for sb in range(lo // 128, hi // 128):
                first.setdefault(sb, i)
                last[sb] = i

        def emit_qk(i):
            kt, lo, hi, kind = tiles[i]
            sbk = cnt["s"] % 3
            cnt["s"] += 1
            terms = [qk_fn(kt, lo, hi)] + extra_fn(kt, lo, hi, kind)
            bufs = []
            for (_, _, bb) in terms:
                bufs += bb

            def mm(e, terms=terms, sbk=sbk, lo=lo, hi=hi):
                ins = None
                for j, (lt, rh, _) in enumerate(terms):
                    ins = e.matmul(PS[sbk][0:kparts, lo:hi], lhsT=lt, rhs=rh, start=(j == 0), stop=(j == len(terms) - 1))
                return ins
            PE(mm, bufs, [BPS[sbk]])
            return sbk

        def emit_exp_pv(i, sbk):
            kt, lo, hi, kind = tiles[i]
            pi = cnt["w"] % 3
            cnt["w"] += 1
            pt = bfv(o_pt[pi], 512)
            ACT((lambda pt, sbk, lo, hi: (lambda e: e.activation(out=pt[0:kparts, lo:hi], in_=PS[sbk][0:kparts, lo:hi], func=AF.Exp)))(pt, sbk, lo, hi),
                [BPS[sbk]], [Bpt[pi]])

            def pv(e, pt=pt, lo=lo, hi=hi, kt=kt, i=i):
                ins = None
                sbs = list(range(lo // 128, hi // 128))
                for sb in sbs:
                    ins = e.matmul(PS[ob][:, sb * W:(sb + 1) * W], lhsT=pt[0:kparts, sb * 128:(sb + 1) * 128], rhs=vrhs_fn(kt),
                                   start=(i == 0 and sb == sbs[0]), stop=(i == n - 1 and sb == sbs[-1]))
                return ins
            PE(pv, [Bpt[pi]] + vbufs, [BPS[ob]])
        sb_prev = emit_qk(0)
        for i in range(n):
            sb_next = emit_qk(i + 1) if i + 1 < n else None
            emit_exp_pv(i, sb_prev)
            sb_prev = sb_next
        return ob

    def finalize(ob, W, t, o_sm, Bsm, gate_ap, o_acc_ap_fn, Bacc, firstbr, imp=None):
        Ov = PS[ob][:, 0:4 * W].rearrange("p (s w) -> p s w", s=4)
        rd = A[:, o_sm:o_sm + 4]
        DVE(lambda e: e.tensor_scalar(out=rd, in0=Ov[:, :, 64], scalar1=1e-30, scalar2=None, op0=ALU.max), [BPS[ob]], [Bsm])
        DVE(lambda e: e.reciprocal(out=rd, in_=rd), [Bsm], [Bsm])
        if imp is not None:
            o_imp, Bimp, firsth = imp
            for sb in range(4):
                ia = A[:, o_imp + sb * 32:o_imp + (sb + 1) * 32]
                if firsth:
                    DVE((lambda sb, ia: (lambda e: e.tensor_scalar(out=ia, in0=Ov[:, sb, 65:97], scalar1=rd[:, sb:sb + 1], scalar2=None, op0=ALU.mult)))(sb, ia),
                        [BPS[ob], Bsm], [Bimp])
                else:
                    DVE((lambda sb, ia: (lambda e: e.scalar_tensor_tensor(out=ia, in0=Ov[:, sb, 65:97], scalar=rd[:, sb:sb + 1], in1=ia, op0=ALU.mult, op1=ALU.add)))(sb, ia),
                        [BPS[ob], Bsm, Bimp], [Bimp])
        if gate_ap is not None:
            DVE(lambda e: e.tensor_tensor(out=rd, in0=rd, in1=gate_ap, op=ALU.mult), [Bsm] + gate_ap_bufs, [Bsm])
        for sb in range(4):
            dst = o_acc_ap_fn(sb)
            if firstbr:
                DVE((lambda sb, dst: (lambda e: e.tensor_scalar(out=dst, in0=Ov[:, sb, 0:64], scalar1=rd[:, sb:sb + 1], scalar2=None, op0=ALU.mult)))(sb, dst),
                    [BPS[ob], Bsm], [Bacc])
            else:
                DVE((lambda sb, dst: (lambda e: e.scalar_tensor_tensor(out=dst, in0=Ov[:, sb, 0:64], scalar=rd[:, sb:sb + 1], in1=dst, op0=ALU.mult, op1=ALU.add)))(sb, dst),
                    [BPS[ob], Bsm, Bacc], [Bacc])
    gate_ap_bufs = []

    def out_proj(t, o_acc, Bacc, nch, chunk0, o_oT, BoT):
        ncols = nch * 128
        for ch in range(nch):
            def tr(e, ch=ch):
                ins = None
                for sb in range(4):
                    ins = e.transpose(PS[6][:, sb * 128:(sb + 1) * 128], A[:, o_acc + sb * ncols + ch * 128:o_acc + sb * ncols + (ch + 1) * 128], c.ident_f)
                return ins
            PE(tr, [Bacc, c.Bcf], [BPS[6]])
            ACT((lambda ch: (lambda e: e.activation(out=bfv(o_oT[ch], 512), in_=PS[6][:, :], func=AF.Identity)))(ch), [BPS[6]], [BoT[ch]])
        slot = c.ring_load([(lambda o: A[:, o:o + 1024].bitcast(BF16).rearrange("p (k n) -> p k n", k=2)[:, 0:nch, :],
                             c.wout_d[l, chunk0 * 128:(chunk0 + nch) * 128, :].rearrange("(k p) n -> p k n", p=128))])
        for n in range(8):
            bank = 3 if n % 2 == 0 else 7

            def mm(e, n=n, bank=bank):
                ins = None
                for ch in range(nch):
                    ins = e.matmul(PS[bank][:, :], lhsT=c.ringv(slot, 128, ch * 1024 + n * 128), rhs=bfv(o_oT[ch], 512), start=(ch == 0), stop=(ch == nch - 1))
                return ins
            PE(mm, [c.Bring[slot]] + [BoT[ch] for ch in range(nch)], [BPS[bank]])
            DVE((lambda n, bank: (lambda e: e.scalar_tensor_tensor(out=c.xT(n, t), in0=PS[bank][:, :], scalar=c.modv(li, 1, 2, s)[:, n:n + 1], in1=c.xT(n, t),
                                                                    op0=ALU.mult, op1=ALU.add)))(n, bank), [BPS[bank], c.BX(n, t), c.Bmodp], [c.BX(n, t)])

    def diag_extra(kind, lo, hi):
        if kind == "diag":
            return [(c.ident_b, c.cbv("tri", 0, hi - lo), [c.Bcb])]
        if kind.startswith("low"):
            i = int(kind[3:])
            return [(c.ident_b, c.cbv("m2", 384 - 128 * i, 128 * (i + 1)), [c.Bcb])]
        return []

    for g in range(2):
        ra = make_region()
        o_q = [ra(T * 2) for _ in range(2)]
        o_kc, o_ks, o_kw, o_vc = ra(T * 2), ra(T * 2), ra(T * 2), ra(T * 2)
        o_vs, o_vw = ra(16 * 65 * 2), ra(16 * 65 * 2)
        o_gt = ra(16 * 12 * 4)
        o_pt = [ra(1024) for _ in range(3)]
        o_acc = ra(4 * 256 * 4)
        o_oT = [ra(1024) for _ in range(2)]
        o_rb = [ra(1024) for _ in range(2)]
        o_selT = ra(1024)
        o_imp = ra(4 * 32 * 4)
        o_imp2 = ra(4 * 32 * 4)
        o_m8 = ra(32 * 4)
        o_selb = ra(4 * 32 * 4)
        o_sm = ra(16)
        o_hid = [[ra(256) for _ in range(2)] for _ in range(2)]
        o_kcmp = ra(256)
        o_w2k = ra(2 * 128 * 2)
        o_w2v = ra(2 * 64 * 2)
        o_posT = [ra(64) for _ in range(2)]
        o_bcol = ra(16)
        o_ovl = ra(256)
        DBG.update(n_acc=o_acc, n_oT=o_oT, n_imp2=o_imp2, n_selT=o_selT, n_end=ra(0))
        Bq = [RB("nq0"), RB("nq1")]
        Bkc, Bks, Bkw, Bvc, Bvs, Bvw, Bgt = [RB(n) for n in "nkc nks nkw nvc nvs nvw ngt".split()]
        Bpt = [RB("npt%d" % i) for i in range(3)]
        Bacc, Bsel, Bimp, Bimp2, Bm8, Bselb, Bsm = [RB(n) for n in "nacc nsel nimp nimp2 nm8 nselb nsm".split()]
        BoT = [RB("noT0"), RB("noT1")]
        Brb = [RB("nrb0"), RB("nrb1")]
        Bhid = [[RB("nhid%d%d" % (a_, b_)) for b_ in range(2)] for a_ in range(2)]
        Bkcmp, Bw2k, Bw2v, Bbcol, Bovl = [RB(n) for n in "nkcmp nw2k nw2v nbcol novl".split()]
        BposT = [RB("nposT0"), RB("nposT1")]
        for (o_v, Bv) in ((o_vs, Bvs), (o_vw, Bvw)):
            DVE((lambda o_v: (lambda e: e.memset(bfv(o_v, 16 * 65), 1.0)))(o_v), [], [Bv])
        DVE(lambda e: e.tensor_copy(out=bfv(o_ovl, 128), in_=c.cbv("ovl")), [c.Bcb], [Bovl])
        sl_q = c.ring_load([(slotdst(0, 256), win_src(C_NQ + g * 256, 256))])
        sl_k = c.ring_load([(slotdst(0, 64), win_src(C_KC + g * 64, 64)), (slotdst(64, 64), win_src(C_KC + g * 64, 64)),
                            (slotdst(128, 64), win_src(C_KS + g * 64, 64)), (slotdst(192, 64), win_src(C_KS + g * 64, 64))])
        for t in range(4):
            for ch in range(2):
                b = cnt["p"] % 4
                proj_fm(sl_q, ch * 128, 128, t, b)
                rope_evac(b, 0.125, bfv(o_q[ch], 512, c0=t * 512), Bq[ch], t, o_rb, Brb)
        for t in range(4):
            for (c0, o_k, Bk) in ((0, o_kc, Bkc), (128, o_ks, Bks)):
                b = cnt["p"] % 4
                proj_fm(sl_k, c0, 128, t, b)
                rope_evac(b, 1.0, bfv(o_k, 512, c0=t * 512), Bk, t, o_rb, Brb)
        sl_k2 = c.ring_load([(slotdst(0, 64), win_src(C_KW + g * 64, 64)), (slotdst(64, 64), win_src(C_KW + g * 64, 64)),
                             (slotdst(128, 64), win_src(C_VC + g * 64, 64)), (slotdst(192, 12), win_src(C_NG + g * 12, 12))])
        for t in range(4):
            b = cnt["p"] % 4
            proj_fm(sl_k2, 0, 128, t, b)
            rope_evac(b, 1.0, bfv(o_kw, 512, c0=t * 512), Bkw, t, o_rb, Brb)
            b = cnt["p"] % 4
            cnt["p"] += 1
            proj_fm(sl_k2, 128, 64, t, b)
            plain_evac(b, 64, 1.0, bfv(o_vc, 512, 0, 64, c0=t * 512), Bvc)
        for tt in range(16):
            b = cnt["p"] % 4
            cnt["p"] += 1
            proj_tm(sl_k2, 192, 12, tt, b)
            ACT((lambda tt, b: (lambda e: e.activation(out=A[:, o_gt + tt * 12:o_gt + (tt + 1) * 12], in_=PS[b][:, 0:12], func=AF.Sigmoid)))(tt, b), [BPS[b]], [Bgt])
        sl_v = c.ring_load([(slotdst(0, 64), win_src(C_VS + g * 64, 64)), (slotdst(64, 64), win_src(C_VW + g * 64, 64))])
        for tt in range(16):
            b = cnt["p"] % 4
            cnt["p"] += 1
            proj_tm(sl_v, 0, 128, tt, b)
            ACT((lambda tt, b: (lambda e: e.activation(out=bfv(o_vs, 64, c0=tt * 65), in_=PS[b][:, 0:64], func=AF.Identity)))(tt, b), [BPS[b]], [Bvs])
            ACT((lambda tt, b: (lambda e: e.activation(out=bfv(o_vw, 64, c0=tt * 65), in_=PS[b][:, 64:128], func=AF.Identity)))(tt, b), [BPS[b]], [Bvw])
        S.dma("pool", lambda e: e.dma_start(out=A[:, o_w2k:o_w2k + 128].bitcast(BF16).rearrange("p (k n) -> p k n", k=2)[:, :, 0:64],
                                            in_=c.cw2_d[l, 0].rearrange("(k p) n -> p k n", p=128)), "dq1", (), [Bw2k])
        S.dma("pool", lambda e: e.dma_start(out=A[:, o_w2k:o_w2k + 128].bitcast(BF16).rearrange("p (k n) -> p k n", k=2)[:, :, 64:128],
                                            in_=c.cw2_d[l, 0].rearrange("(k p) n -> p k n", p=128)), "dq1", (), [Bw2k])
        S.dma("pool", lambda e: e.dma_start(out=A[:, o_w2v:o_w2v + 64].bitcast(BF16).rearrange("p (k n) -> p k n", k=2),
                                            in_=c.cw2_d[l, 1].rearrange("(k p) n -> p k n", p=128)), "dq2", (), [Bw2v])
        for which in range(2):
            S.dma("pool", (lambda which: (lambda e: e.dma_start(out=A[0:64, o_posT[which]:o_posT[which] + 16].bitcast(BF16),
                                                                in_=c.cpos_d[l, which].rearrange("l d -> d l"), allow_slow_non_contiguous=True)))(which),
                  "dq3" if which == 0 else "dq4", (), [BposT[which]])
        for which in range(2):
            src_o, Bsrc = (o_kc, Bkc) if which == 0 else (o_vc, Bvc)
            for lg in range(4):
                slw = c.ring_load([(lambda o: A[0:64, o:o + 1024].bitcast(BF16).rearrange("p (k n) -> p k n", k=8),
                                    c.cw1_d[l, which, lg * 512:(lg + 1) * 512, :].rearrange("(a d) n -> d a n", d=64))])

                def mm(e, slw=slw, lg=lg, src_o=src_o, which=which):
                    ins = None
                    for hc in range(2):
                        for ll in range(8):
                            lq = lg * 8 + ll
                            w = c.ringv(slw, 128, ll * 256 + hc * 128, 0, 64)
                            base = src_o + lq // 2
                            rhs = A[0:64, src_o:src_o + T // 2].bitcast(BF16)[:, lq:lq + 16 * 126 + 1:16]
                            ins = e.matmul(PS[hc][:, 0:127], lhsT=w, rhs=rhs, start=(lq == 0), stop=(lq == 31))
                            ins = e.matmul(PS[2 + hc][:, 0:1], lhsT=w, rhs=A[0:64, o_posT[which]:o_posT[which] + 16].bitcast(BF16)[:, lq:lq + 1],
                                           start=(lq == 0), stop=(lq == 31))
                    return ins
                PE(mm, [c.Bring[slw], Bsrc, BposT[which]], [BPS[0], BPS[1], BPS[2], BPS[3]])
            for hc in range(2):
                DVE((lambda hc: (lambda e: e.tensor_copy(out=A[:, o_bcol + hc:o_bcol + hc + 1], in_=PS[2 + hc][:, 0:1])))(hc), [BPS[2 + hc]], [Bbcol])
                ACT((lambda hc, which: (lambda e: e.activation(out=bfv(o_hid[which][hc], 127), in_=PS[hc][:, 0:127], func=AF.Gelu_apprx_tanh,
                                                                bias=A[:, o_bcol + hc:o_bcol + hc + 1])))(hc, which), [BPS[hc], Bbcol], [Bhid[which][hc]])
            if which == 0:
                def mm2(e):
                    ins = None
                    for hc in range(2):
                        ins = e.matmul(PS[4][:, 0:127], lhsT=bfv(o_w2k, 128, c0=hc * 128), rhs=bfv(o_hid[0][hc], 127), start=(hc == 0), stop=(hc == 1))
                    return ins
                PE(mm2, [Bw2k, Bhid[0][0], Bhid[0][1]], [BPS[4]])
                ACT(lambda e: e.activation(out=bfv(o_kcmp, 127), in_=PS[4][:, 0:127], func=AF.Identity), [BPS[4]], [Bkcmp])
            else:
                def mm3(e):
                    ins = None
                    for hc in range(2):
                        ins = e.matmul(PS[5][0:127, 0:64], lhsT=bfv(o_hid[1][hc], 127), rhs=bfv(o_w2v, 64, c0=hc * 64), start=(hc == 0), stop=(hc == 1))
                    return ins
                PE(mm3, [Bw2v, Bhid[1][0], Bhid[1][1]], [BPS[5]])
                ACT(lambda e: e.activation(out=bfv(o_ovl, 64, 0, 127), in_=PS[5][0:127, 0:64], func=AF.Identity), [BPS[5]], [Bovl])
        for t in range(4):
            gate_ap_bufs[:] = [Bgt]

            def gate_view(r, br):
                a = o_gt + t * 48 + r * 3 + br
                return A[:, a:a + 37:12]

            def accv(r):
                return lambda sb: A[:, o_acc + sb * 256 + r * 64:o_acc + sb * 256 + (r + 1) * 64]
            for r in range(4):
                ch, p0 = r // 2, (r % 2) * 64
                qk = lambda kt, lo, hi, ch=ch, p0=p0: (bfv(o_kcmp, 127, p0, p0 + 64), bfv(o_q[ch], hi - lo, p0, p0 + 64, c0=t * 512 + lo), [Bkcmp, Bq[ch]])
                ex = lambda kt, lo, hi, kind: [(c.cbv("ident", 0, 127, 0, 127) if False else c.ident_b[0:127, 0:127], c.cbv("cmpmask", t * 512 + lo, hi - lo, 0, 127), [c.Bcb])]
                ob = attend(t, [(0, 0, 512, "cmp")], qk, ex, 97, lambda kt: bfv(o_ovl, 97, 0, 127), [Bovl], o_pt, Bpt, kparts=127)
                finalize(ob, 97, t, o_sm, Bsm, gate_view(r, 0), accv(r), Bacc, True, imp=(o_imp, Bimp, r == 0))
            iv = c.cfv("impvalid", t * 128, 128)
            ib = c.cfv("impbias", t * 128, 128)
            imp_ap = A[:, o_imp:o_imp + 128]
            imp2 = A[:, o_imp2:o_imp2 + 128]
            DVE((lambda iv: (lambda e: e.tensor_tensor(out=imp2, in0=imp_ap, in1=iv, op=ALU.mult)))(iv), [Bimp, c.Bcf], [Bimp2])
            DVE((lambda ib: (lambda e: e.tensor_tensor(out=imp2, in0=imp2, in1=ib, op=ALU.add)))(ib), [Bimp2, c.Bcf], [Bimp2])
            for sb in range(4):
                DVE((lambda sb: (lambda e: e.max(out=A[:, o_m8 + sb * 8:o_m8 + (sb + 1) * 8], in_=A[:, o_imp2 + sb * 32:o_imp2 + (sb + 1) * 32])))(sb), [Bimp2], [Bm8])
            for sb in range(4):
                DVE((lambda sb: (lambda e: e.tensor_scalar(out=A[:, o_selb + sb * 32:o_selb + (sb + 1) * 32], in0=A[:, o_imp2 + sb * 32:o_imp2 + (sb + 1) * 32],
                                                           scalar1=A[:, o_m8 + sb * 8 + 7:o_m8 + sb * 8 + 8], scalar2=NEG, op0=ALU.is_lt, op1=ALU.mult)))(sb),
                    [Bimp2, Bm8], [Bselb])

            def trs(e):
                ins = None
                for sb in range(4):
                    ins = e.transpose(PS[6][0:32, sb * 128:(sb + 1) * 128], A[:, o_selb + sb * 32:o_selb + (sb + 1) * 32], c.ident_f)
                return ins
            PE(trs, [Bselb, c.Bcf], [BPS[6]])
            ACT(lambda e: e.activation(out=bfv(o_selT, 512, 0, 32), in_=PS[6][0:32, :], func=AF.Identity), [BPS[6]], [Bsel])
            for br, (o_k, Bk, o_v, Bv) in ((1, (o_ks, Bks, o_vs, Bvs)), (2, (o_kw, Bkw, o_vw, Bvw))):
                tiles = causal_tiles(t) if br == 1 else window_tiles(t)
                for r in range(4):
                    ch, p0 = r // 2, (r % 2) * 64
                    qk = lambda kt, lo, hi, ch=ch, p0=p0, o_k=o_k, Bk=Bk: (bfv(o_k, 128, p0, p0 + 64, c0=kt * 128), bfv(o_q[ch], hi - lo, p0, p0 + 64, c0=t * 512 + lo), [Bk, Bq[ch]])
                    if br == 1:
                        ex = lambda kt, lo, hi, kind: [(c.cbv("eslc", kt * 128, 128, 0, 32), bfv(o_selT, hi - lo, 0, 32, c0=lo), [c.Bcb, Bsel])] + diag_extra(kind, lo, hi)
                    else:
                        ex = lambda kt, lo, hi, kind: diag_extra(kind, lo, hi)
                    ob = attend(t, tiles, qk, ex, 65, (lambda o_v: (lambda kt: bfv(o_v, 65, c0=kt * 65)))(o_v), [Bv], o_pt, Bpt)
                    finalize(ob, 65, t, o_sm, Bsm, gate_view(r, br), accv(r), Bacc, False)
            out_proj(t, o_acc, Bacc, 2, 2 * g, o_oT, BoT)
    if stop == "nsa":
        return
    for mix in (("fox",) if "nomoba" in _os2.environ.get("KDBG", "").split(",") else ("fox", "moba")):
        ra = make_region()
        o_q = [ra(T * 2) for _ in range(2)]
        o_k = [ra(T * 2) for _ in range(2)]
        o_v = ra(16 * 4 * 65 * 2)
        o_pt = [ra(1024) for _ in range(3)]
        o_acc = ra(4 * 256 * 4)
        o_oT = [ra(1024) for _ in range(2)]
        o_rb = [ra(1024) for _ in range(2)]
        o_sm = ra(16)
        Bq = [RB(mix + "q0"), RB(mix + "q1")]
        Bk = [RB(mix + "k0"), RB(mix + "k1")]
        Bv, Bacc, Bsm = RB(mix + "v"), RB(mix + "acc"), RB(mix + "sm")
        Bpt = [RB(mix + "pt%d" % i) for i in range(3)]
        BoT = [RB(mix + "oT0"), RB(mix + "oT1")]
        Brb = [RB(mix + "rb0"), RB(mix + "rb1")]
        cq, ck, cv = (C_FQ, C_FK, C_FV) if mix == "fox" else (C_MQ, C_MK, C_MV)
        _dbg = _os2.environ.get("KDBG", "").split(",")
        if not (mix == "moba" and "nomemset" in _dbg):
            DVE(lambda e: e.memset(bfv(o_v, 16 * 4 * 65), 1.0), [], [Bv])
        if mix == "moba" and "noring" in _dbg:
            continue
        if mix == "moba" and "MLOADS" in _os2.environ:
            _cols = [int(v) for v in _os2.environ.get("MCOLS", "").split(",") if v]
            for _i in range(int(_os2.environ["MLOADS"])):
                c.ring_load([(slotdst(0, 256), win_src(_cols[_i] if _i < len(_cols) else int(_os2.environ.get("MCOL", cq)), 256))])
            continue
        sl_q = c.ring_load([(slotdst(0, 256), win_src(cq, 256))])
        sl_k = c.ring_load([(slotdst(0, 256), win_src(ck, 256))])
        _dbg = _os2.environ.get("KDBG", "").split(",")
        for t in range(4):
            if mix == "moba" and "noqk" in _dbg:
                continue
            for ch in range(2):
                for (sl, sc, o_d, Bd) in ((sl_q, 0.125, o_q, Bq), (sl_k, 1.0, o_k, Bk)):
                    b = cnt["p"] % 4
                    proj_fm(sl, ch * 128, 128, t, b)
                    if mix == "moba":
                        rope_evac(b, sc, bfv(o_d[ch], 512, c0=t * 512), Bd[ch], t, o_rb, Brb)
                    else:
                        cnt["p"] += 1
                        plain_evac(b, 128, sc, bfv(o_d[ch], 512, c0=t * 512), Bd[ch])
        sl_v = c.ring_load([(slotdst(0, 256), win_src(cv, 256))])
        for tt in range(16):
            if mix == "moba" and "nov" in _dbg:
                continue
            b = cnt["p"] % 4
            cnt["p"] += 1
            proj_tm(sl_v, 0, 256, tt, b)
            ACT((lambda tt, b: (lambda e: e.activation(out=bfv(o_v, 16 * 260).rearrange("p (a h w) -> p a h w", a=16, h=4)[:, tt, :, 0:64],
                                                       in_=PS[b][:, 0:256].rearrange("p (h w) -> p h w", h=4), func=AF.Identity)))(tt, b), [BPS[b]], [Bv])
        if mix == "fox":
            o_sp, o_G, o_G8, o_nG8, o_glo, o_carry = ra(2048), ra(2048), ra(T * 2), ra(T * 2), ra(1024), ra(16)
            Bsp, BG, BG8, BnG8, Bglo, Bcarry = [RB(n) for n in "fsp fG fG8 fnG8 fglo fcarry".split()]
            sl_f = c.ring_load([(slotdst(0, 4), win_src(C_FF, 4)), (slotdst(4, 4), win_src(C_FF, 4))])
            for t in range(4):
                b = cnt["p"] % 4
                cnt["p"] += 1
                proj_fm(sl_f, 0, 8, t, b)
                sp = A[0:8, o_sp:o_sp + 512]
                G = A[0:8, o_G:o_G + 512]
                ACT((lambda b: (lambda e: e.activation(out=sp, in_=PS[b][0:8, :], func=AF.Exp, scale=-1.0, bias=A[0:8, c.o_fb + l:c.o_fb + l + 1])))(b), [BPS[b], c.Bfb], [Bsp])
                ACT(lambda e: e.activation(out=sp, in_=sp, func=AF.Ln, bias=1.0, scale=1.0), [Bsp], [Bsp])
                if t == 0:
                    DVE(lambda e: e.tensor_tensor_scan(out=G, data0=sp, data1=sp, initial=0.0, op0=ALU.add, op1=ALU.max), [Bsp], [BG])
                else:
                    DVE(lambda e: e.tensor_tensor_scan(out=G, data0=sp, data1=sp, initial=A[0:8, o_carry:o_carry + 1], op0=ALU.add, op1=ALU.max), [Bsp, Bcarry], [BG])
                DVE(lambda e: e.tensor_copy(out=A[0:8, o_carry:o_carry + 1], in_=G[:, 511:512]), [BG], [Bcarry])
                g8 = bfv(o_G8, 512, 0, 8, c0=t * 512)
                ng8 = bfv(o_nG8, 512, 0, 8, c0=t * 512)
                glo = bfv(o_glo, 512, 0, 8)
                DVE((lambda g8: (lambda e: e.tensor_copy(out=g8, in_=G)))(g8), [BG], [BG8])
                DVE((lambda g8: (lambda e: e.tensor_tensor(out=glo, in0=G, in1=g8, op=ALU.subtract)))(g8), [BG, BG8], [Bglo])
                DVE(lambda e: e.tensor_scalar(out=glo, in0=glo, scalar1=c.cfv("mlo", 0, 1, 0, 8), scalar2=None, op0=ALU.mult), [Bglo, c.Bcf], [Bglo])
                DVE((lambda g8: (lambda e: e.scalar_tensor_tensor(out=g8, in0=g8, scalar=c.cfv("mhi", 0, 1, 0, 8), in1=glo, op0=ALU.mult, op1=ALU.add)))(g8),
                    [BG8, Bglo, c.Bcf], [BG8])
                DVE((lambda g8, ng8: (lambda e: e.tensor_scalar(out=ng8, in0=g8, scalar1=-1.0, scalar2=None, op0=ALU.mult)))(g8, ng8), [BG8], [BnG8])
        else:
            o_km, o_kmb, o_g2, o_Mm8, o_Mselb, o_MselT = ra(2 * 8 * 4), ra(2 * 8 * 2), ra(128), ra(128), ra(128), ra(1024)
            Bkm, Bkmb, Bg2, Bm8, Bselb, Bsel = [RB(n) for n in "mkm mkmb mg2 mm8 mselb msel".split()]
            DBG.update(o_oT0=o_oT[0], o_oT1=o_oT[1], o_km=o_km, o_kmb=o_kmb, o_g2=o_g2, o_Mm8=o_Mm8, o_Mselb=o_Mselb, o_MselT=o_MselT, o_sm=o_sm, o_acc=o_acc)
            for ch in range(2):
                if "nokm" in _dbg:
                    continue
                DVE((lambda ch: (lambda e: e.tensor_reduce(out=A[:, o_km + ch * 8:o_km + (ch + 1) * 8], in_=bfv(o_k[ch], T).rearrange("p (b n) -> p b n", b=8),
                                                           axis=AX.X, op=ALU.add)))(ch), [Bk[ch]], [Bkm])
            if "nokmb" not in _dbg:
                DVE(lambda e: e.tensor_copy(out=bfv(o_kmb, 16), in_=A[:, o_km:o_km + 16]), [Bkm], [Bkmb])
        import os as _os
        for t in range(4):
            if mix == "moba" and "nomobaatt" in _os.environ.get("KDBG", "").split(","):
                continue
            accv = lambda r: (lambda sb: A[:, o_acc + sb * 256 + r * 64:o_acc + sb * 256 + (r + 1) * 64])
            for r in range(4):
                ch, p0 = r // 2, (r % 2) * 64
                qk = lambda kt, lo, hi, ch=ch, p0=p0: (bfv(o_k[ch], 128, p0, p0 + 64, c0=kt * 128), bfv(o_q[ch], hi - lo, p0, p0 + 64, c0=t * 512 + lo), [Bk[ch], Bq[ch]])
                if mix == "fox":
                    ex = lambda kt, lo, hi, kind, r=r: [
                        (bfv(o_G8, 128, 0, 8, c0=kt * 128), c.cbv("selq", r * 512, hi - lo, 0, 8), [BG8, c.Bcb]),
                        (c.cbv("sel2", r * 128, 128, 0, 8), bfv(o_nG8, hi - lo, 0, 8, c0=t * 512 + lo), [BnG8, c.Bcb])] + diag_extra(kind, lo, hi)
                else:
                    def gm(e, ch=ch, p0=p0, t=t):
                        ins = None
                        for sb in range(4):
                            ins = e.matmul(PS[6][:, sb * 8:(sb + 1) * 8], lhsT=bfv(o_q[ch], 128, p0, p0 + 64, c0=t * 512 + sb * 128),
                                           rhs=bfv(o_kmb, 8, p0, p0 + 64, c0=ch * 8), start=True, stop=True)
                        return ins
                    PE(gm, [Bq[ch], Bkmb], [BPS[6]])
                    g2 = A[:, o_g2:o_g2 + 32]
                    DVE((lambda t: (lambda e: e.tensor_tensor(out=g2, in0=PS[6][:, 0:32], in1=c.cfv("mobabias", t * 32, 32), op=ALU.add)))(t), [BPS[6], c.Bcf], [Bg2])
                    for sb in range(4):
                        DVE((lambda sb: (lambda e: e.max(out=A[:, o_Mm8 + sb * 8:o_Mm8 + (sb + 1) * 8], in_=A[:, o_g2 + sb * 8:o_g2 + (sb + 1) * 8])))(sb), [Bg2], [Bm8])
                    for sb in range(4):
                        DVE((lambda sb: (lambda e: e.tensor_scalar(out=A[:, o_Mselb + sb * 8:o_Mselb + (sb + 1) * 8], in0=A[:, o_g2 + sb * 8:o_g2 + (sb + 1) * 8],
                                                                   scalar1=A[:, o_Mm8 + sb * 8 + 3:o_Mm8 + sb * 8 + 4], scalar2=NEG, op0=ALU.is_lt, op1=ALU.mult)))(sb),
                            [Bg2, Bm8], [Bselb])

                    def trs(e):
                        ins = None
                        for sb in range(4):
                            ins = e.transpose(PS[6][0:8, sb * 128:(sb + 1) * 128], A[:, o_Mselb + sb * 8:o_Mselb + (sb + 1) * 8], c.ident_f)
                        return ins
                    PE(trs, [Bselb, c.Bcf], [BPS[6]])
                    ACT(lambda e: e.activation(out=bfv(o_MselT, 512, 0, 8), in_=PS[6][0:8, :], func=AF.Identity), [BPS[6]], [Bsel])
                    ex = lambda kt, lo, hi, kind: [(c.cbv("emoba", kt * 128, 128, 0, 8), bfv(o_MselT, hi - lo, 0, 8, c0=lo), [c.Bcb, Bsel])] + diag_extra(kind, lo, hi)
                ob = attend(t, causal_tiles(t), qk, ex, 65, (lambda r: (lambda kt: bfv(o_v, 65, c0=(kt * 4 + r) * 65)))(r), [Bv], o_pt, Bpt)
                finalize(ob, 65, t, o_sm, Bsm, None, accv(r), Bacc, True)
            import os as _os
            if not (mix == "moba" and "nomobaout" in _os.environ.get("KDBG", "")):
                out_proj(t, o_acc, Bacc, 2, 4 if mix == "fox" else 6, o_oT, BoT)
        if stop == "fox" and mix == "fox":
            return


def make_inputs(inputs, core, nseq=NSEQ):
    b0 = core * nseq
    cb, cf = make_consts()
    prow = np.zeros((512, 128), np.float32)
    prow[0:288] = np.asarray(inputs["b_ada"], np.float32).reshape(288, 128)
    prow[288:384] = np.asarray(inputs["norm_g"], np.float32).reshape(96, 128)
    prow[384:392] = np.asarray(inputs["final_g"], np.float32).reshape(8, 128)
    prow[392:392 + 8 * nseq] = np.asarray(inputs["c"], np.float32)[b0:b0 + nseq].reshape(8 * nseq, 128)
    fb = np.asarray(inputs["fox_fbias"], np.float32)
    fb8 = np.concatenate([fb.T, fb.T], axis=0).copy()
    m = {
        "x": np.ascontiguousarray(np.asarray(inputs["x"], np.float32)[b0:b0 + nseq]),
        "pos": np.ascontiguousarray(np.asarray(inputs["positions"], np.int32)[b0:b0 + nseq]),
        "prow": prow, "fb8": fb8, "cb": cb, "cf": cf,
    }
    for k in ("w_ada", "w_in", "cmp_pos", "cmp_w1", "cmp_w2", "w_out", "ffn_w13", "ffn_w2"):
        m[k] = np.asarray(inputs[k], np.float32)
    return m


_CACHE = {}


def kernel(**inputs):
    ncores = 8
    key = "full"
    if key not in _CACHE:
        _CACHE[key] = build_program(list(range(DEPTH)), final_norm=True)
    nc = _CACHE[key]
    in_maps = [make_inputs(inputs, c) for c in range(ncores)]
    res = run_bass_kernel_spmd(nc, in_maps, core_ids=list(range(ncores)))
    out = np.concatenate([np.asarray(r["y"], np.float32) for r in res.results], axis=0)
    return out
```
